# Optimizing a Trainium2 kernel written in Bass

```python
import math
import jax, jax.numpy as jnp
from jax import lax
import numpy as np

D_MODEL = 1024
BATCH = 2
SEQ = 8192
DEPTH = 2

N_A = DEPTH // 2
N_B = DEPTH - N_A

H_A = 4
DK_A = D_MODEL // H_A
DV_A = 2 * DK_A
WIDTH_A = H_A * DV_A
CHUNK = 128
ROPE_BASE_A = 10000.0

H_B = 16
QK_NOPE = 128
QK_ROPE = 64
V_HEAD = 128
Q_LORA = 768
KV_LORA = 512
WIDTH_B = H_B * V_HEAD
Q_BLOCK = 128
ROPE_BASE_B = 10000.0

IN_A = H_A * DK_A * 2 + WIDTH_A + WIDTH_A
IN_B = Q_LORA + WIDTH_B

DEEPNORM_ALPHA = (2.0 * DEPTH) ** 0.25
DEEPNORM_BETA = (8.0 * DEPTH) ** -0.25

kernel_name = "yoco_retention_mla_gated_deepnorm"


def _layer_norm(x, g, b, eps=1e-5):
    xf = x.astype(jnp.float32)
    mu = jnp.mean(xf, axis=-1, keepdims=True)
    var = jnp.mean(jnp.square(xf - mu), axis=-1, keepdims=True)
    return ((xf - mu) * lax.rsqrt(var + eps)).astype(x.dtype) * g + b


def _rms_norm(x, g, eps=1e-6):
    xf = x.astype(jnp.float32)
    return (xf * lax.rsqrt(jnp.mean(jnp.square(xf), axis=-1, keepdims=True) + eps)).astype(x.dtype) * g


def _rope(x, base):
    s, d = x.shape[1], x.shape[-1]
    half = d // 2
    inv = base ** (-jnp.arange(half, dtype=jnp.float32) / half)
    ang = jnp.arange(s, dtype=jnp.float32)[:, None] * inv[None, :]
    shp = (s,) + (1,) * (x.ndim - 3) + (half,)
    cos = jnp.cos(ang).reshape(shp).astype(x.dtype)
    sin = jnp.sin(ang).reshape(shp).astype(x.dtype)
    x1, x2 = x[..., :half], x[..., half:]
    return jnp.concatenate([x1 * cos - x2 * sin, x2 * cos + x1 * sin], axis=-1)


def _chunkwise_retention(q, k, v):
    b, s, h, dk = q.shape
    dv = v.shape[-1]
    n = s // CHUNK

    def to_chunks(t):
        return t.astype(jnp.float32).reshape(b, n, CHUNK, h, t.shape[-1]).transpose(1, 0, 3, 2, 4)

    qc, kc, vc = to_chunks(q), to_chunks(k), to_chunks(v)
    lg = jnp.log1p(-jnp.exp2(-5.0 - jnp.arange(h, dtype=jnp.float32)))
    idx = jnp.arange(CHUNK, dtype=jnp.float32)
    diff = idx[:, None] - idx[None, :]
    causal = diff >= 0
    dmat = jnp.where(causal, jnp.exp(jnp.where(causal, diff, 0.0)[None] * lg[:, None, None]), 0.0)
    qdec = jnp.exp((idx + 1.0)[None, :] * lg[:, None])
    kdec = jnp.exp((CHUNK - 1.0 - idx)[None, :] * lg[:, None])
    cdec = jnp.exp(CHUNK * lg)

    def step(state, inp):
        qi, ki, vi = inp
        scores = jnp.einsum('bhcd,bhed->bhce', qi, ki) * dmat
        inner = jnp.einsum('bhce,bhev->bhcv', scores, vi)
        cross = jnp.einsum('bhcd,bhdv->bhcv', qi * qdec[None, :, :, None], state)
        state = state * cdec[None, :, None, None] + jnp.einsum(
            'bhcd,bhcv->bhdv', ki * kdec[None, :, :, None], vi)
        return state, inner + cross

    init = jnp.zeros((b, h, dk, dv), jnp.float32)
    _, out = lax.scan(step, init, (qc, kc, vc))
    return out.transpose(1, 0, 3, 2, 4).reshape(b, s, h, dv)


def _retention_layer(x, w_in, w_out):
    b, s, _ = x.shape
    hproj = x @ w_in
    qk = H_A * DK_A
    q = hproj[..., :qk].reshape(b, s, H_A, DK_A)
    k = hproj[..., qk:2 * qk].reshape(b, s, H_A, DK_A)
    v = hproj[..., 2 * qk:2 * qk + WIDTH_A].reshape(b, s, H_A, DV_A)
    gate = hproj[..., 2 * qk + WIDTH_A:]
    q = _rope(q, ROPE_BASE_A)
    k = _rope(k, ROPE_BASE_A) * (DK_A ** -0.5)
    o = _chunkwise_retention(q, k, v)
    mu = jnp.mean(o, axis=-1, keepdims=True)
    var = jnp.mean(jnp.square(o - mu), axis=-1, keepdims=True)
    o = ((o - mu) * lax.rsqrt(var + 1e-5)).astype(x.dtype).reshape(b, s, WIDTH_A)
    return (o * jax.nn.silu(gate)) @ w_out


def _shared_latent_kv(x, w_down, kv_norm, w_up):
    b, s, _ = x.shape
    c = x @ w_down
    lat = _rms_norm(c[..., :KV_LORA], kv_norm)
    k_rope = _rope(c[..., KV_LORA:], ROPE_BASE_B)
    kv = (lat @ w_up).reshape(b, s, H_B, QK_NOPE + V_HEAD)
    return kv[..., :QK_NOPE], k_rope, kv[..., QK_NOPE:]


def _causal_block_attention(q_nope, q_rope, k_nope, k_rope, v):
    b, s, h, _ = q_nope.shape
    nb = s // Q_BLOCK
    scale = (QK_NOPE + QK_ROPE) ** -0.5
    qn_b = q_nope.reshape(b, nb, Q_BLOCK, h, QK_NOPE).transpose(1, 0, 2, 3, 4)
    qr_b = q_rope.reshape(b, nb, Q_BLOCK, h, QK_ROPE).transpose(1, 0, 2, 3, 4)
    kpos = jnp.arange(s)

    def block(args):
        qn_i, qr_i, i = args
        sc = (jnp.einsum('bqhd,bkhd->bhqk', qn_i, k_nope)
              + jnp.einsum('bqhr,bkr->bhqk', qr_i, k_rope)).astype(jnp.float32) * scale
        qpos = i * Q_BLOCK + jnp.arange(Q_BLOCK)
        sc = jnp.where(kpos[None, :] <= qpos[:, None], sc, -jnp.inf)
        p = jax.nn.softmax(sc, axis=-1).astype(v.dtype)
        return jnp.einsum('bhqk,bkhd->bqhd', p, v)

    out = lax.map(block, (qn_b, qr_b, jnp.arange(nb)))
    return out.transpose(1, 0, 2, 3, 4).reshape(b, s, h, V_HEAD)


def _mla_layer(x, w_in, q_norm, w_uq, w_out, k_nope, k_rope, v):
    b, s, _ = x.shape
    hproj = x @ w_in
    q_lat, gate = hproj[..., :Q_LORA], hproj[..., Q_LORA:]
    q = (_rms_norm(q_lat, q_norm) @ w_uq).reshape(b, s, H_B, QK_NOPE + QK_ROPE)
    q_nope = q[..., :QK_NOPE]
    q_rope = _rope(q[..., QK_NOPE:], ROPE_BASE_B)
    o = _causal_block_attention(q_nope, q_rope, k_nope, k_rope, v).reshape(b, s, WIDTH_B)
    return (o * jax.nn.silu(gate)) @ w_out


def setup_inputs(seed: int = 0) -> dict:
    key = jax.random.key(seed)
    ks = jax.random.split(key, 16)
    nrm = jax.random.normal
    f32 = jnp.float32
    return {
        "x": nrm(ks[0], (BATCH, SEQ, D_MODEL), f32),
        "a_w_in": nrm(ks[1], (N_A, D_MODEL, IN_A), f32) * D_MODEL ** -0.5,
        "a_w_out": nrm(ks[2], (N_A, WIDTH_A, D_MODEL), f32) * (WIDTH_A ** -0.5 * DEEPNORM_BETA),
        "b_w_in": nrm(ks[3], (N_B, D_MODEL, IN_B), f32) * D_MODEL ** -0.5,
        "b_q_norm": 1.0 + 0.02 * nrm(ks[4], (N_B, Q_LORA), f32),
        "b_w_uq": nrm(ks[5], (N_B, Q_LORA, H_B * (QK_NOPE + QK_ROPE)), f32) * Q_LORA ** -0.5,
        "b_w_out": nrm(ks[6], (N_B, WIDTH_B, D_MODEL), f32) * (WIDTH_B ** -0.5 * DEEPNORM_BETA),
        "kv_w_down": nrm(ks[7], (D_MODEL, KV_LORA + QK_ROPE), f32) * D_MODEL ** -0.5,
        "kv_norm": 1.0 + 0.02 * nrm(ks[8], (KV_LORA,), f32),
        "kv_w_up": nrm(ks[9], (KV_LORA, H_B * (QK_NOPE + V_HEAD)), f32) * KV_LORA ** -0.5,
        "ln_g": 1.0 + 0.02 * nrm(ks[10], (DEPTH, D_MODEL), f32),
        "ln_b": 0.02 * nrm(ks[11], (DEPTH, D_MODEL), f32),
    }


def reference(x, a_w_in, a_w_out, b_w_in, b_q_norm, b_w_uq, b_w_out,
              kv_w_down, kv_norm, kv_w_up, ln_g, ln_b):
    shared = None
    for layer in range(DEPTH):
        if layer < N_A:
            y = _retention_layer(x, a_w_in[layer], a_w_out[layer])
        else:
            j = layer - N_A
            if j == 0:
                shared = _shared_latent_kv(x, kv_w_down, kv_norm, kv_w_up)
            y = _mla_layer(x, b_w_in[j], b_q_norm[j], b_w_uq[j], b_w_out[j], *shared)
        x = _layer_norm(DEEPNORM_ALPHA * x + y, ln_g[layer], ln_b[layer])
    return x
```

```python
import contextlib
import math
import numpy as np
import ml_dtypes
import concourse.bass as bass
import concourse.mybir as mybir
from concourse.bass_utils import run_bass_kernel_spmd

F32 = mybir.dt.float32
BF16 = mybir.dt.bfloat16
AF = mybir.ActivationFunctionType
ALU = mybir.AluOpType

D = 1024
S = 8192
NB = 2
ALPHA = 4.0 ** 0.25
NCORES = 8


class Buf:
    __slots__ = ("t", "w", "r", "name")

    def __init__(self, t, name):
        self.t = t
        self.w = None
        self.r = []
        self.name = name

    def __getitem__(self, idx):
        return self.t[idx]


class Prog:
    ENG = ("sp", "pe", "act", "dve", "pool")

    def __init__(self, nc, stack):
        self.nc = nc
        self.stack = stack
        self.q = {e: [] for e in self.ENG}
        self.sem = {e: stack.enter_context(nc.semaphore("s_" + e)) for e in ("pe", "act", "dve", "pool")}
        self.cnt = {e: 0 for e in self.sem}
        self.seen = {e: {} for e in self.ENG}
        self.dsem = {}
        self.dcnt = {}
        self.n = 0

    def sb(self, name, shape, dt):
        return Buf(self.nc.alloc_sbuf_tensor("sb_" + name, shape, dt), name)

    def ps(self, name):
        return Buf(self.nc.alloc_psum_tensor("ps_" + name, [128, 512], F32), name)

    def dram(self, name, shape, dt, kind):
        return Buf(self.nc.dram_tensor(name, shape, dt, kind=kind).ap(), name)

    def view(self, buf_or_ap, name):
        return Buf(buf_or_ap, name)

    def _deps(self, eng, reads, writes):
        toks = []
        for b in reads:
            if b.w is not None:
                toks.append(b.w)
        for b in writes:
            if b.w is not None:
                toks.append(b.w)
            toks.extend(b.r)
        best = {}
        for (s, v) in toks:
            k = id(s)
            if k not in best or best[k][1] < v:
                best[k] = (s, v)
        out = []
        seen = self.seen[eng]
        for k, (s, v) in best.items():
            if seen.get(k, 0) >= v:
                continue
            seen[k] = v
            out.append((s, v))
        return out

    def _commit(self, tok, reads, writes):
        for b in reads:
            b.r.append(tok)
            if len(b.r) > 24:
                best = {}
                for (s, v) in b.r:
                    if id(s) not in best or best[id(s)][1] < v:
                        best[id(s)] = (s, v)
                b.r = list(best.values())
        for b in writes:
            b.w = tok
            b.r = []

    def op(self, eng, fns, reads=(), writes=()):
        if not isinstance(fns, (list, tuple)):
            fns = [fns]
        waits = self._deps(eng, reads, writes)
        self.cnt[eng] += 1
        tok = (self.sem[eng], self.cnt[eng])
        self.q[eng].append((waits, fns, tok, 1))
        self._commit(tok, reads, writes)
        return tok

    def dma(self, fn, reads=(), writes=(), key=None, q="sp"):
        waits = self._deps(q, reads, writes)
        if key is None:
            key = "dma"
        if key not in self.dsem:
            self.dsem[key] = self.stack.enter_context(self.nc.semaphore("d_%d" % len(self.dsem)))
            self.dcnt[key] = 0
        self.dcnt[key] += 16
        tok = (self.dsem[key], self.dcnt[key])
        self.q[q].append((waits, [fn], tok, 16))
        self._commit(tok, reads, writes)
        return tok

    def wait_all(self, eng, toks):
        best = {}
        for (s, v) in toks:
            if id(s) not in best or best[id(s)][1] < v:
                best[id(s)] = (s, v)
        self.q[eng].append((list(best.values()), [], None, 0))

    def emit(self):
        with self.nc.Block() as block:
            decos = {"sp": block.sync, "pe": block.tensor, "act": block.scalar,
                     "dve": block.vector, "pool": block.gpsimd}
            for eng in self.ENG:
                items = self.q[eng]

                def body(e, items=items):
                    for (waits, fns, tok, inc) in items:
                        for (s, v) in waits:
                            e.wait_ge(s, v)
                        last = None
                        for f in fns:
                            last = f(e)
                        if tok is not None and last is not None:
                            last.then_inc(tok[0], inc)

                decos[eng](body)


def _act(out, in_, func, **kw):
    return lambda e: e.activation(out=out, in_=in_, func=func, **kw)


def _tt(out, a, b, op):
    return lambda e: e.tensor_tensor(out=out, in0=a, in1=b, op=op)


def _ts(out, a, s1, s2, op0, op1=None):
    if op1 is None:
        return lambda e: e.tensor_scalar(out=out, in0=a, scalar1=s1, scalar2=None, op0=op0)
    return lambda e: e.tensor_scalar(out=out, in0=a, scalar1=s1, scalar2=s2, op0=op0, op1=op1)


def _cp(out, in_):
    return lambda e: e.tensor_copy(out=out, in_=in_)


def _mm(out, lhsT, rhs, start, stop):
    return lambda e: e.matmul(out, lhsT, rhs, start=start, stop=stop)


def _tr(out, in_, ident):
    return lambda e: e.transpose(out, in_, ident)


def _dma(out, in_):
    return lambda e: e.dma_start(out=out, in_=in_)


def build_A(nblk=16):
    nc = bass.Bass("TRN2", target_bir_lowering=False)
    with contextlib.ExitStack() as stack:
        P = Prog(nc, stack)
        xT = P.dram("xT", [D, S], F32, "ExternalInput")
        wA = P.dram("wA", [D, 1536], F32, "ExternalInput")
        cosT = P.dram("cosT", [128, S], F32, "ExternalInput")
        sinT = P.dram("sinT", [128, S], F32, "ExternalInput")
        cst = P.dram("cst", [128, 128 + 512 + 128 + 2], F32, "ExternalInput")
        GT = P.dram("GT", [512, S], BF16, "ExternalOutput")

        W = P.sb("W", [128, 8, 1536], BF16)
        wst = [P.sb("wst%d" % i, [128, 1536], F32) for i in range(2)]
        C = P.sb("C", [128, 770], F32)
        identb = P.sb("identb", [128, 128], BF16)
        xs = [P.sb("xs%d" % i, [128, 8, 512], F32) for i in range(2)]
        xb = [P.sb("xb%d" % i, [128, 8, 512], BF16) for i in range(2)]
        cs = [P.sb("cs%d" % i, [128, 512], F32) for i in range(2)]
        sn = [P.sb("sn%d" % i, [128, 512], F32) for i in range(2)]
        qT = [P.sb("qT%d" % i, [128, 2, 512], BF16) for i in range(2)]
        qdT = [P.sb("qdT%d" % i, [128, 2, 512], BF16) for i in range(2)]
        kT = [P.sb("kT%d" % i, [128, 2, 512], BF16) for i in range(2)]
        vb = [P.sb("vb%d" % i, [128, 4, 512], BF16) for i in range(2)]
        sg = [P.sb("sg%d" % i, [128, 4, 512], F32) for i in range(2)]
        gts = [P.sb("gts%d" % i, [128, 4, 512], BF16) for i in range(2)]
        t1 = P.sb("t1", [128, 512], F32)
        t2 = P.sb("t2", [128, 512], F32)
        rf = P.sb("rf", [128, 2, 512], F32)
        kd = [P.sb("kd%d" % i, [128, 256], BF16) for i in range(2)]
        sTd = [P.sb("sTd%d" % i, [128, 128], BF16) for i in range(2)]
        Sf = P.sb("Sf", [128, 2, 512], F32)
        Sb = P.sb("Sb", [128, 2, 512], BF16)
        stt = P.sb("stt", [128, 6], F32)
        mv = P.sb("mv", [128, 2], F32)
        rstd = P.sb("rstd", [128, 1], F32)
        on = P.sb("on", [128, 512], F32)
        Gb = [P.sb("Gb%d" % i, [128, 512], BF16) for i in range(2)]

        pj = [P.ps("pj%d" % i) for i in range(2)]
        msc = [P.ps("msc%d" % i) for i in range(2)]
        mS = [P.view(msc[i][:, 0:128], "mS%d" % i) for i in range(2)]
        mK = [P.view(msc[i][:, 128:256].bitcast(BF16), "mK%d" % i) for i in range(2)]
        mG = [P.view(msc[i][:, 256:512].bitcast(BF16), "mG%d" % i) for i in range(2)]
        ops = [P.ps("o%d" % i) for i in range(2)]
        upd = [P.ps("upd%d" % i) for i in range(2)]

        dmatT = C[:, 0:128]
        qdec = C[:, 128:640]
        kdec = C[:, 768:769]
        cdec = C[:, 769:770]

        P.dma(_dma(C[:, :], cst[:, :]), writes=[C], key="C")
        P.op("dve", _cp(identb[:, :], C[:, 640:768]), reads=[C], writes=[identb])
        P.op("dve", lambda e: e.memset(Sf[:, :, :], 0.0), writes=[Sf])
        P.op("dve", lambda e: e.memset(Sb[:, :, :], 0.0), writes=[Sb])
        for kc in range(8):
            w_ = wst[kc % 2]
            P.dma(_dma(w_[:, :], wA[kc * 128:(kc + 1) * 128, :]), writes=[w_], key="wst%d" % (kc % 2))
            P.op("dve" if kc % 2 == 0 else "pool", _cp(W[:, kc, :], w_[:, :]), reads=[w_], writes=[W])

        xTv = xT.t.rearrange("(k p) t -> p k t", p=128)
        GTv = GT.t.rearrange("(f p) t -> p f t", p=128)
        pjn = [0]

        def load(j):
            s = j % 2
            P.dma(_dma(xs[s][:, :, :], xTv[:, :, j * 512:(j + 1) * 512]), writes=[xs[s]], key="xs%d" % s)
            P.dma(_dma(cs[s][:, :], cosT[:, j * 512:(j + 1) * 512]), writes=[cs[s]], key="cs%d" % s)
            P.dma(_dma(sn[s][:, :], sinT[:, j * 512:(j + 1) * 512]), writes=[sn[s]], key="sn%d" % s)

        def proj(j):
            s = j % 2
            P.op("pool", _cp(xb[s][:, 0:4, :], xs[s][:, 0:4, :]), reads=[xs[s]], writes=[xb[s]])
            P.op("pool", _cp(xb[s][:, 4:8, :], xs[s][:, 4:8, :]), reads=[xs[s]], writes=[xb[s]])
            for which in range(2):
                banks = []
                for oc in range(2):
                    pb = pj[pjn[0] % 2]
                    pjn[0] += 1
                    col = which * 256 + oc * 128
                    P.op("pe", [_mm(pb[:, :], W[:, kc, col:col + 128], xb[s][:, kc, :], kc == 0, kc == 7)
                                for kc in range(8)], reads=[W, xb[s]], writes=[pb])
                    banks.append(pb)
                p0, p1 = banks
                dst = qT[s] if which == 0 else kT[s]
                P.op("dve", _tt(t1[:, :], p0[:, :], cs[s][:, :], ALU.mult), reads=[p0, cs[s]], writes=[t1])
                P.op("dve", _tt(t2[:, :], p1[:, :], sn[s][:, :], ALU.mult), reads=[p1, sn[s]], writes=[t2])
                P.op("pool", _tt(rf[:, 0, :], t1[:, :], t2[:, :], ALU.subtract), reads=[t1, t2], writes=[rf])
                P.op("dve", _tt(t1[:, :], p1[:, :], cs[s][:, :], ALU.mult), reads=[p1, cs[s]], writes=[t1])
                P.op("dve", _tt(t2[:, :], p0[:, :], sn[s][:, :], ALU.mult), reads=[p0, sn[s]], writes=[t2])
                P.op("pool", _tt(rf[:, 1, :], t1[:, :], t2[:, :], ALU.add), reads=[t1, t2], writes=[rf])
                P.op("pool", _cp(dst[:, :, :], rf[:, :, :]), reads=[rf], writes=[dst])
                if which == 0:
                    for oc in range(2):
                        P.op("pool", _tt(qdT[s][:, oc, :], rf[:, oc, :], qdec, ALU.mult),
                             reads=[rf, C], writes=[qdT[s]])
            for c in range(4):
                pb = pj[pjn[0] % 2]
                pjn[0] += 1
                P.op("pe", [_mm(pb[:, :], xb[s][:, kc, c * 128:(c + 1) * 128], W[:, kc, 512:1024], kc == 0, kc == 7)
                            for kc in range(8)], reads=[W, xb[s]], writes=[pb])
                P.op("act", _act(vb[s][:, c, :], pb[:, :], AF.Copy), reads=[pb], writes=[vb[s]])
                pb = pj[pjn[0] % 2]
                pjn[0] += 1
                P.op("pe", [_mm(pb[:, :], xb[s][:, kc, c * 128:(c + 1) * 128], W[:, kc, 1024:1536], kc == 0, kc == 7)
                            for kc in range(8)], reads=[W, xb[s]], writes=[pb])
                P.op("act", _act(sg[s][:, c, :], pb[:, :], AF.Silu), reads=[pb], writes=[sg[s]])

        def ts_(j, c):
            s = j % 2
            g = (j * 4 + c) % 2
            sl = slice(c * 128, (c + 1) * 128)
            P.op("pe", [_tr(mK[g][:, dc * 128:(dc + 1) * 128], kT[s][:, dc, sl], identb[:, :]) for dc in range(2)],
                 reads=[kT[s], identb], writes=[mK[g]])
            P.op("act", _act(kd[g][:, :], mK[g][:, :], AF.Copy, scale=kdec), reads=[mK[g], C], writes=[kd[g]])
            P.op("pe", [_mm(mS[g][:, :], kT[s][:, dc, sl], qT[s][:, dc, sl], dc == 0, dc == 1) for dc in range(2)],
                 reads=[kT[s], qT[s]], writes=[mS[g]])
            P.op("dve", _tt(sTd[g][:, :], mS[g][:, :], dmatT, ALU.mult), reads=[mS[g], C], writes=[sTd[g]])

        def ret(j, c):
            s = j % 2
            g = (j * 4 + c) % 2
            sl = slice(c * 128, (c + 1) * 128)
            o = ops[g]
            P.op("pe", [_mm(o[:, :], sTd[g][:, :], vb[s][:, c, :], True, False),
                        _mm(o[:, :], qdT[s][:, 0, sl], Sb[:, 0, :], False, False),
                        _mm(o[:, :], qdT[s][:, 1, sl], Sb[:, 1, :], False, True)],
                 reads=[sTd[g], vb[s], qdT[s], Sb], writes=[o])
            for dc in range(2):
                P.op("pe", _mm(upd[dc][:, :], kd[g][:, dc * 128:(dc + 1) * 128], vb[s][:, c, :], True, True),
                     reads=[kd[g], vb[s]], writes=[upd[dc]])
            for dc in range(2):
                P.op("dve", lambda e, dc=dc: e.scalar_tensor_tensor(
                    out=Sf[:, dc, :], in0=Sf[:, dc, :], scalar=cdec, in1=upd[dc][:, :],
                    op0=ALU.mult, op1=ALU.add), reads=[Sf, upd[dc], C], writes=[Sf])
            P.op("act", _act(Sb[:, :, :], Sf[:, :, :], AF.Copy), reads=[Sf], writes=[Sb])
            P.op("dve", lambda e: e.bn_stats(out=stt[:, :], in_=o[:, :]), reads=[o], writes=[stt])
            P.op("dve", lambda e: e.bn_aggr(out=mv[:, :], in_=stt[:, :]), reads=[stt], writes=[mv])
            P.op("act", _act(rstd[:, :], mv[:, 1:2], AF.Sqrt, bias=1e-5, scale=1.0), reads=[mv], writes=[rstd])
            P.op("dve", lambda e: e.reciprocal(out=rstd[:, :], in_=rstd[:, :]), reads=[rstd], writes=[rstd])
            P.op("dve", _ts(on[:, :], o[:, :], mv[:, 0:1], rstd[:, 0:1], ALU.subtract, ALU.mult),
                 reads=[o, mv, rstd], writes=[on])
            P.op("pool", _tt(Gb[g][:, :], on[:, :], sg[s][:, c, :], ALU.mult), reads=[on, sg[s]], writes=[Gb[g]])

        def gtr(j, c):
            s = j % 2
            g = (j * 4 + c) % 2
            P.op("pe", [_tr(mG[g][:, f * 128:(f + 1) * 128], Gb[g][:, f * 128:(f + 1) * 128], identb[:, :])
                        for f in range(4)], reads=[Gb[g], identb], writes=[mG[g]])
            P.op("act", _act(gts[s][:, :, c * 128:(c + 1) * 128],
                             mG[g][:, :].rearrange("p (f t) -> p f t", f=4), AF.Copy), reads=[mG[g]], writes=[gts[s]])
            if c == 3:
                P.dma(_dma(GTv[:, :, j * 512:(j + 1) * 512], gts[s][:, :, :]), reads=[gts[s]], writes=[GT],
                      key="gts%d" % s)

        load(0)
        if nblk > 1:
            load(1)
        proj(0)
        pend = None
        for j in range(nblk):
            if j + 1 < nblk:
                proj(j + 1)
            if j + 2 < nblk:
                load(j + 2)
            ts_(j, 0)
            for c in range(4):
                if c + 1 < 4:
                    ts_(j, c + 1)
                ret(j, c)
                if pend is not None:
                    gtr(*pend)
                pend = (j, c)
        gtr(*pend)
        P.wait_all("sp", [GT.w])
        P.emit()
    return nc


def _rope_tables(half, base=10000.0):
    inv = (np.float32(base) ** (-np.arange(half, dtype=np.float32) / np.float32(half))).astype(np.float32)
    ang = (np.arange(S, dtype=np.float32)[:, None] * inv[None, :]).astype(np.float32)
    return np.cos(ang).astype(np.float32), np.sin(ang).astype(np.float32)


def _consts_A(h):
    lg = np.log1p(-np.exp2(np.float32(-5.0 - h))).astype(np.float32)
    idx = np.arange(128, dtype=np.float32)
    diff = idx[:, None] - idx[None, :]
    dmat = np.where(diff >= 0, np.exp(np.where(diff >= 0, diff, 0.0) * lg), 0.0).astype(np.float32)
    qdec = np.exp((idx + 1.0) * lg).astype(np.float32)
    kdec = np.exp((127.0 - idx) * lg).astype(np.float32)
    cdec = np.exp(np.float32(128.0) * lg).astype(np.float32)
    c = np.zeros((128, 770), np.float32)
    c[:, 0:128] = dmat.T / 16.0
    c[:, 128:640] = np.tile(qdec, 4)[None, :]
    c[:, 640:768] = np.eye(128, dtype=np.float32)
    c[:, 768] = kdec / 16.0
    c[:, 769] = cdec
    return c


def run_A(x, a_w_in, nblk=16):
    cos, sin = _rope_tables(128)
    cosT = np.ascontiguousarray(cos.T)
    sinT = np.ascontiguousarray(sin.T)
    w = a_w_in[0]
    in_maps = []
    for core in range(NCORES):
        b, h = divmod(core, 4)
        wA = np.concatenate([w[:, h * 256:(h + 1) * 256], w[:, 1024 + h * 256:1024 + (h + 1) * 256],
                             w[:, 2048 + h * 512:2048 + (h + 1) * 512],
                             w[:, 4096 + h * 512:4096 + (h + 1) * 512]], axis=1)
        in_maps.append({"xT": np.ascontiguousarray(x[b].T), "wA": np.ascontiguousarray(wA),
                        "cosT": cosT, "sinT": sinT, "cst": _consts_A(h)})
    nc = build_A(nblk)
    res = run_bass_kernel_spmd(nc, in_maps, core_ids=list(range(NCORES)))
    return [r["GT"] for r in res.results]


def _outproj_ln(P, T, gt, tsl, wo, xres, gvec, bvec, dst):
    for half in range(2):
        P.op("pe", [_mm(T["yp"][half][:, :], gt[:, kc, tsl], wo[:, kc, half * 512:(half + 1) * 512], kc == 0, kc == 15)
                    for kc in range(16)], reads=[gt, wo], writes=[T["yp"][half]])
    z = T["z"]
    for half in range(2):
        hs = slice(half * 512, (half + 1) * 512)
        P.op("dve", lambda e, half=half, hs=hs: e.scalar_tensor_tensor(
            out=z[:, hs], in0=xres[:, hs], scalar=ALPHA, in1=T["yp"][half][:, :], op0=ALU.mult, op1=ALU.add),
            reads=[xres, T["yp"][half]], writes=[z])
        P.op("dve", lambda e, half=half, hs=hs: e.bn_stats(out=T["stt"][:, half * 6:(half + 1) * 6], in_=z[:, hs]),
             reads=[z], writes=[T["stt"]])
    P.op("dve", lambda e: e.bn_aggr(out=T["mv"][:, :], in_=T["stt"][:, :]), reads=[T["stt"]], writes=[T["mv"]])
    P.op("act", _act(T["rstd"][:, :], T["mv"][:, 1:2], AF.Sqrt, bias=1e-5, scale=1.0), reads=[T["mv"]], writes=[T["rstd"]])
    P.op("dve", lambda e: e.reciprocal(out=T["rstd"][:, :], in_=T["rstd"][:, :]), reads=[T["rstd"]], writes=[T["rstd"]])
    P.op("dve", _ts(z[:, :], z[:, :], T["mv"][:, 0:1], T["rstd"][:, 0:1], ALU.subtract, ALU.mult),
         reads=[z, T["mv"], T["rstd"]], writes=[z])
    P.op("pool", _tt(z[:, :], z[:, :], gvec, ALU.mult), reads=[z, T["vec"]], writes=[z])
    P.op("pool", _tt(dst[:, :], z[:, :], bvec, ALU.add), reads=[z, T["vec"]], writes=[dst])


def _load_w(P, dst, src, nk, ncol, wst, engs=("dve", "pool")):
    n = 0
    for kc in range(nk):
        for c0 in range(0, ncol, 1024):
            c1 = min(ncol, c0 + 1024)
            w_ = wst[n % 2]
            P.dma(_dma(w_[:, 0:c1 - c0], src[kc * 128:(kc + 1) * 128, c0:c1]), writes=[w_], key="wst%d" % (n % 2))
            P.op(engs[n % 2], _cp(dst[:, kc, c0:c1], w_[:, 0:c1 - c0]), reads=[w_], writes=[dst])
            n += 1


TB = 2048


def build_B(ngrp=4):
    nc = bass.Bass("TRN2", target_bir_lowering=False)
    with contextlib.ExitStack() as stack:
        P = Prog(nc, stack)
        GTin = P.dram("GTin", [2048, TB], BF16, "ExternalInput")
        xres = P.dram("xres", [TB, D], F32, "ExternalInput")
        wo_d = P.dram("wo", [2048, D], F32, "ExternalInput")
        wd_d = P.dram("wd", [D, 576], F32, "ExternalInput")
        wi_d = P.dram("wi", [D, 2816], F32, "ExternalInput")
        vec_d = P.dram("vec", [128, 1024 + 1024 + 512 + 768 + 128], F32, "ExternalInput")
        ck_d = P.dram("ck", [TB, 64], F32, "ExternalInput")
        x1_d = P.dram("x1", [TB, D], F32, "ExternalOutput")
        sg_d = P.dram("sgT", [2048, TB], BF16, "ExternalOutput")
        qn_d = P.dram("qnT", [768, TB], BF16, "ExternalOutput")
        lat_d = P.dram("latT", [512, TB], BF16, "ExternalOutput")
        kr_d = P.dram("krT", [64, TB], BF16, "ExternalOutput")

        wo = P.sb("wo", [128, 16, 1024], BF16)
        wd = P.sb("wd", [128, 8, 576], BF16)
        wi = P.sb("wi", [128, 8, 2816], BF16)
        wst = [P.sb("wst%d" % i, [128, 1024], F32) for i in range(2)]
        vec = P.sb("vec", [128, 3456], F32)
        identf = vec[:, 3328:3456]
        identb = P.sb("identb", [128, 128], BF16)
        ck = P.sb("ck", [128, 16, 64], F32)
        gtb = [P.sb("gtb%d" % i, [128, 16, 512], BF16) for i in range(2)]
        xs = [P.sb("xs", [128, 1024], F32)] * 2
        x1s = [P.sb("x1s%d" % i, [128, 1024], F32) for i in range(2)]
        x1T = [P.sb("x1T%d" % i, [128, 8, 512], BF16) for i in range(2)]
        T = {"z": P.sb("z", [128, 1024], F32), "stt": P.sb("stt", [128, 12], F32), "mv": P.sb("mv", [128, 2], F32),
             "rstd": P.sb("rstd", [128, 1], F32), "vec": vec}
        ss = P.sb("ss", [128, 4], F32)
        rs = P.sb("rs", [128, 2], F32)
        junk = P.sb("junk", [128, 512], F32)
        latn = P.sb("latn", [128, 512], BF16)
        qn = P.sb("qn", [128, 768], BF16)
        krb = P.sb("krb", [128, 64], BF16)
        kt1 = P.sb("kt1", [128, 64], F32)
        kt2 = P.sb("kt2", [128, 64], F32)
        latTs = [P.sb("latTs", [128, 4, 512], BF16)] * 2
        qnTs = [P.sb("qnTs", [128, 6, 512], BF16)] * 2
        krTs = [P.sb("krTs", [64, 512], BF16)] * 2
        sgs = [P.sb("sgs", [128, 16, 512], BF16)] * 2

        banks = [P.ps("b%d" % i) for i in range(8)]
        T["yp"] = [banks[0], banks[1]]
        xtp = [banks[2], banks[3]]
        latp, ql0 = banks[4], banks[5]
        ql1 = P.view(banks[6][:, 0:256], "ql1")
        krp = P.view(banks[6][:, 256:320], "krp")
        trp = P.view(banks[7][:, :].bitcast(BF16), "trp")

        P.dma(_dma(vec[:, :], vec_d[:, :]), writes=[vec], key="vec")
        P.dma(_dma(ck[:, :, :], ck_d.t.rearrange("(t p) c -> p t c", p=128)), writes=[ck], key="ck")
        P.op("dve", _cp(identb[:, :], identf), reads=[vec], writes=[identb])
        _load_w(P, wo, wo_d, 16, 1024, wst)
        _load_w(P, wd, wd_d, 8, 576, wst)
        _load_w(P, wi, wi_d, 8, 2816, wst)
        gvec, bvec = vec[:, 0:1024], vec[:, 1024:2048]
        kvn, qnv = vec[:, 2048:2560], vec[:, 2560:3328]

        GTv = GTin.t.rearrange("(k p) t -> p k t", p=128)
        sgv = sg_d.t.rearrange("(k p) t -> p k t", p=128)
        qnv_d = qn_d.t.rearrange("(k p) t -> p k t", p=128)
        latv_d = lat_d.t.rearrange("(k p) t -> p k t", p=128)

        for gi in range(ngrp):
            s = gi % 2
            gsl = slice(gi * 512, (gi + 1) * 512)
            P.dma(_dma(gtb[s][:, :, :], GTv[:, :, gsl]), writes=[gtb[s]], key="gtb%d" % s)
            for t in range(4):
                ti = gi * 4 + t
                u = ti % 2
                tsl = slice(t * 128, (t + 1) * 128)
                P.dma(_dma(xs[u][:, :], xres[ti * 128:(ti + 1) * 128, :]), writes=[xs[u]], key="xs%d" % u)
                _outproj_ln(P, T, gtb[s], tsl, wo, xs[u], gvec, bvec, x1s[u])
                P.dma(_dma(x1_d[ti * 128:(ti + 1) * 128, :], x1s[u][:, :]), reads=[x1s[u]], writes=[x1_d], key="x1s%d" % u)
                for hb in range(2):
                    P.op("pe", [_tr(xtp[hb][:, q * 128:(q + 1) * 128], x1s[u][:, (hb * 4 + q) * 128:(hb * 4 + q + 1) * 128],
                                    identf) for q in range(4)], reads=[x1s[u], vec], writes=[xtp[hb]])
                    P.op("act", _act(x1T[s][:, hb * 4:(hb + 1) * 4, tsl],
                                     xtp[hb][:, :].rearrange("p (q t) -> p q t", q=4), AF.Copy),
                         reads=[xtp[hb]], writes=[x1T[s]])
                P.op("pe", [_mm(latp[:, :], x1T[s][:, kc, tsl], wd[:, kc, 0:512], kc == 0, kc == 7) for kc in range(8)],
                     reads=[x1T[s], wd], writes=[latp])
                P.op("pe", [_mm(krp[:, :], x1T[s][:, kc, tsl], wd[:, kc, 512:576], kc == 0, kc == 7) for kc in range(8)],
                     reads=[x1T[s], wd], writes=[krp])
                P.op("pe", [_mm(ql0[:, :], x1T[s][:, kc, tsl], wi[:, kc, 0:512], kc == 0, kc == 7) for kc in range(8)],
                     reads=[x1T[s], wi], writes=[ql0])
                P.op("pe", [_mm(ql1[:, :], x1T[s][:, kc, tsl], wi[:, kc, 512:768], kc == 0, kc == 7) for kc in range(8)],
                     reads=[x1T[s], wi], writes=[ql1])
                P.op("act", _act(junk[:, :], latp[:, :], AF.Square, accum_out=ss[:, 0:1]), reads=[latp], writes=[junk, ss])
                P.op("act", _act(junk[:, :], ql0[:, :], AF.Square, accum_out=ss[:, 1:2]), reads=[ql0], writes=[junk, ss])
                P.op("act", _act(junk[:, 0:256], ql1[:, :], AF.Square, accum_out=ss[:, 2:3]), reads=[ql1], writes=[junk, ss])
                P.op("dve", _tt(ss[:, 3:4], ss[:, 1:2], ss[:, 2:3], ALU.add), reads=[ss], writes=[ss])
                P.op("act", _act(rs[:, 0:1], ss[:, 0:1], AF.Sqrt, bias=1e-6, scale=1.0 / 512), reads=[ss], writes=[rs])
                P.op("act", _act(rs[:, 1:2], ss[:, 3:4], AF.Sqrt, bias=1e-6, scale=1.0 / 768), reads=[ss], writes=[rs])
                P.op("dve", lambda e: e.reciprocal(out=rs[:, :], in_=rs[:, :]), reads=[rs], writes=[rs])
                P.op("dve", lambda e: e.scalar_tensor_tensor(out=latn[:, :], in0=latp[:, :], scalar=rs[:, 0:1], in1=kvn,
                                                             op0=ALU.mult, op1=ALU.mult), reads=[latp, rs, vec], writes=[latn])
                P.op("dve", lambda e: e.scalar_tensor_tensor(out=qn[:, 0:512], in0=ql0[:, :], scalar=rs[:, 1:2],
                                                             in1=qnv[:, 0:512], op0=ALU.mult, op1=ALU.mult),
                     reads=[ql0, rs, vec], writes=[qn])
                P.op("dve", lambda e: e.scalar_tensor_tensor(out=qn[:, 512:768], in0=ql1[:, :], scalar=rs[:, 1:2],
                                                             in1=qnv[:, 512:768], op0=ALU.mult, op1=ALU.mult),
                     reads=[ql1, rs, vec], writes=[qn])
                cosk, sink = ck[:, ti, 0:32], ck[:, ti, 32:64]
                P.op("dve", _tt(kt1[:, 0:32], krp[:, 0:32], cosk, ALU.mult), reads=[krp, ck], writes=[kt1])
                P.op("dve", _tt(kt1[:, 32:64], krp[:, 32:64], cosk, ALU.mult), reads=[krp, ck], writes=[kt1])
                P.op("dve", _tt(kt2[:, 0:32], krp[:, 32:64], sink, ALU.mult), reads=[krp, ck], writes=[kt2])
                P.op("dve", _tt(kt2[:, 32:64], krp[:, 0:32], sink, ALU.mult), reads=[krp, ck], writes=[kt2])
                P.op("pool", _tt(krb[:, 0:32], kt1[:, 0:32], kt2[:, 0:32], ALU.subtract), reads=[kt1, kt2], writes=[krb])
                P.op("pool", _tt(krb[:, 32:64], kt1[:, 32:64], kt2[:, 32:64], ALU.add), reads=[kt1, kt2], writes=[krb])
                P.op("pe", [_tr(trp[:, q * 128:(q + 1) * 128], latn[:, q * 128:(q + 1) * 128], identb[:, :]) for q in range(4)]
                     + [_tr(trp[0:64, 512:640], krb[:, :], identb[:, :])], reads=[latn, krb, identb], writes=[trp])
                P.op("act", _act(latTs[s][:, :, tsl], trp[:, 0:512].rearrange("p (q t) -> p q t", q=4), AF.Copy),
                     reads=[trp], writes=[latTs[s]])
                P.op("act", _act(krTs[s][:, tsl], trp[0:64, 512:640], AF.Copy), reads=[trp], writes=[krTs[s]])
                P.op("pe", [_tr(trp[:, q * 128:(q + 1) * 128], qn[:, q * 128:(q + 1) * 128], identb[:, :]) for q in range(6)],
                     reads=[qn, identb], writes=[trp])
                P.op("act", _act(qnTs[s][:, :, tsl], trp[:, 0:768].rearrange("p (q t) -> p q t", q=6), AF.Copy),
                     reads=[trp], writes=[qnTs[s]])
            P.dma(_dma(latv_d[:, :, gsl], latTs[s][:, :, :]), reads=[latTs[s]], writes=[lat_d], key="latTs%d" % s)
            P.dma(_dma(qnv_d[:, :, gsl], qnTs[s][:, :, :]), reads=[qnTs[s]], writes=[qn_d], key="qnTs%d" % s)
            P.dma(_dma(kr_d[:, gsl], krTs[s][:, :]), reads=[krTs[s]], writes=[kr_d], key="krTs%d" % s)
            for fc in range(16):
                pb = T["yp"][fc % 2]
                P.op("pe", [_mm(pb[:, :], wi[:, kc, 768 + fc * 128:768 + (fc + 1) * 128], x1T[s][:, kc, :], kc == 0, kc == 7)
                            for kc in range(8)], reads=[wi, x1T[s]], writes=[pb])
                P.op("act", _act(sgs[s][:, fc, :], pb[:, :], AF.Silu), reads=[pb], writes=[sgs[s]])
            P.dma(_dma(sgv[:, :, gsl], sgs[s][:, :, :]), reads=[sgs[s]], writes=[sg_d], key="sgs%d" % s)
        P.wait_all("sp", [x1_d.w, sg_d.w, qn_d.w, lat_d.w, kr_d.w])
        P.emit()
    return nc


def _bcast(v):
    return np.ascontiguousarray(np.broadcast_to(np.asarray(v, np.float32)[None, :], (128, v.shape[0])))


def run_B(GTs, x, a_w_out, kv_w_down, b_w_in, ln_g, ln_b, kv_norm, b_q_norm, ngrp=4):
    cos, sin = _rope_tables(32)
    vec = np.concatenate([_bcast(ln_g[0]), _bcast(ln_b[0]), _bcast(kv_norm), _bcast(b_q_norm[0]),
                          np.eye(128, dtype=np.float32)], axis=1)
    in_maps = []
    for core in range(NCORES):
        b, j = divmod(core, 4)
        tsl = slice(j * TB, (j + 1) * TB)
        GTin = np.concatenate([np.asarray(GTs[b * 4 + h])[:, tsl] for h in range(4)], axis=0)
        in_maps.append({"GTin": np.ascontiguousarray(GTin), "xres": np.ascontiguousarray(x[b, tsl]),
                        "wo": np.ascontiguousarray(a_w_out[0]), "wd": np.ascontiguousarray(kv_w_down),
                        "wi": np.ascontiguousarray(b_w_in[0]), "vec": vec,
                        "ck": np.ascontiguousarray(np.concatenate([cos[tsl], sin[tsl]], axis=1))})
    nc = build_B(ngrp)
    res = run_bass_kernel_spmd(nc, in_maps, core_ids=list(range(NCORES)))
    return res.results


SCALE = 192.0 ** -0.5


def build_C(nblk=16, npair=2):
    nc = bass.Bass("TRN2", target_bir_lowering=False)
    with contextlib.ExitStack() as stack:
        P = Prog(nc, stack)
        qn_d = P.dram("qnT", [768, S], BF16, "ExternalInput")
        lat_d = P.dram("latT", [512, S], BF16, "ExternalInput")
        kr_d = P.dram("krT", [64, S], BF16, "ExternalInput")
        wq_d = P.dram("wq", [768, 1024], F32, "ExternalInput")
        wu_d = P.dram("wu", [512, 1024], F32, "ExternalInput")
        cq_d = P.dram("cq", [64, S], F32, "ExternalInput")
        sq_d = P.dram("sq", [64, S], F32, "ExternalInput")
        msk_d = P.dram("msk", [128, 256], F32, "ExternalInput")
        oT_d = P.dram("oT", [512, S], F32, "ExternalOutput")

        wq = P.sb("wq", [128, 6, 1024], BF16)
        wu = P.sb("wu", [128, 4, 1024], BF16)
        wst = [P.sb("wst%d" % i, [128, 1024], F32) for i in range(2)]
        mskf = P.sb("mskf", [128, 256], F32)
        mskb = P.sb("mskb", [128, 256], BF16)
        knT = P.sb("knT", [128, 2, S], BF16)
        krT = P.sb("krT", [64, S], BF16)
        V = P.sb("V", [128, 64, 256], BF16)
        qnb = [P.sb("qnb%d" % i, [128, 6, 512], BF16) for i in range(2)]
        latb = [P.sb("latb%d" % i, [128, 4, 512], BF16) for i in range(2)]
        cqb = [P.sb("cqb%d" % i, [64, 512], F32) for i in range(2)]
        sqb = [P.sb("sqb%d" % i, [64, 512], F32) for i in range(2)]
        qT = [P.sb("qT%d" % i, [128, 512], BF16) for i in range(2)]
        qrT = [P.sb("qrT%d" % i, [64, 512], BF16) for i in range(2)]
        ra = P.sb("ra", [64, 512], F32)
        rb = P.sb("rb", [64, 512], F32)
        pT = [P.sb("pT%d" % i, [128, 512], BF16) for i in range(3)]
        rsum = P.sb("rsum", [128, 512], F32)
        ost = [P.sb("ost%d" % i, [128, 512], F32) for i in range(2)]

        sp = [P.ps("s%d" % i) for i in range(2)]
        op_ = [P.ps("o%d" % i) for i in range(2)]
        sm = [P.ps("m%d" % i) for i in range(2)]
        gp = [P.ps("g%d" % i) for i in range(2)]
        gn = [0]

        def gbank():
            b = gp[gn[0] % 2]
            gn[0] += 1
            return b

        P.dma(_dma(mskf[:, :], msk_d[:, :]), writes=[mskf], key="msk")
        P.op("dve", _cp(mskb[:, :], mskf[:, :]), reads=[mskf], writes=[mskb])
        tri, ones = mskb[:, 0:128], mskb[:, 128:256]
        _load_w(P, wq, wq_d, 6, 1024, wst)
        _load_w(P, wu, wu_d, 4, 1024, wst)
        P.dma(_dma(krT[:, :], kr_d[:, :]), writes=[krT], key="krT")

        qnv = qn_d.t.rearrange("(k p) t -> p k t", p=128)
        latv = lat_d.t.rearrange("(k p) t -> p k t", p=128)
        oTv = oT_d.t.rearrange("(h p) t -> p h t", p=128)
        nld = [0]
        pn = [0]
        on_ = [0]

        def load(hp, i):
            s = nld[0] % 2
            nld[0] += 1
            bsl = slice(i * 512, (i + 1) * 512)
            P.dma(_dma(qnb[s][:, :, :], qnv[:, :, bsl]), writes=[qnb[s]], key="qnb%d" % s)
            P.dma(_dma(latb[s][:, :, :], latv[:, :, bsl]), writes=[latb[s]], key="latb%d" % s)
            P.dma(_dma(cqb[s][:, :], cq_d[:, bsl]), writes=[cqb[s]], key="cqb%d" % s)
            P.dma(_dma(sqb[s][:, :], sq_d[:, bsl]), writes=[sqb[s]], key="sqb%d" % s)
            return s

        seq = [(hp, i) for hp in range(npair) for i in range(nblk)]
        slots = {}
        slots[seq[0]] = load(*seq[0])
        for n, (hp, i) in enumerate(seq):
            if n + 1 < len(seq):
                slots[seq[n + 1]] = load(*seq[n + 1])
            s = slots[(hp, i)]
            bsl = slice(i * 512, (i + 1) * 512)
            for hl in range(2):
                h = 2 * hp + hl
                pb = gbank()
                P.op("pe", [_mm(pb[:, :], wu[:, kc, h * 128:(h + 1) * 128], latb[s][:, kc, :], kc == 0, kc == 3)
                            for kc in range(4)], reads=[wu, latb[s]], writes=[pb])
                P.op("dve", _cp(knT[:, hl, bsl], pb[:, :]), reads=[pb], writes=[knT])
            for half in range(2):
                pb = gbank()
                for tt in range(2):
                    t4 = half * 2 + tt
                    P.op("pe", [_mm(pb[:, tt * 256:(tt + 1) * 256], latb[s][:, kc, t4 * 128:(t4 + 1) * 128],
                                    wu[:, kc, 512 + hp * 256:512 + (hp + 1) * 256], kc == 0, kc == 3)
                                for kc in range(4)], reads=[wu, latb[s]], writes=[pb])
                P.op("dve", _cp(V[:, i * 4 + half * 2:i * 4 + half * 2 + 2, :],
                                pb[:, :].rearrange("p (t c) -> p t c", t=2)), reads=[pb], writes=[V])
            for hl in range(2):
                h = 2 * hp + hl
                u = on_[0] % 2
                on_[0] += 1
                pb = gbank()
                P.op("pe", [_mm(pb[:, :], wq[:, kc, h * 256:h * 256 + 128], qnb[s][:, kc, :], kc == 0, kc == 5)
                            for kc in range(6)], reads=[wq, qnb[s]], writes=[pb])
                P.op("dve", _cp(qT[u][:, :], pb[:, :]), reads=[pb], writes=[qT[u]])
                pa = gbank()
                P.op("pe", [_mm(pa[0:64, :], wq[:, kc, h * 256 + 128:h * 256 + 192], qnb[s][:, kc, :], kc == 0, kc == 5)
                            for kc in range(6)], reads=[wq, qnb[s]], writes=[pa])
                P.op("dve", _tt(ra[:, :], pa[0:64, :], cqb[s][:, :], ALU.mult), reads=[pa, cqb[s]], writes=[ra])
                pb2 = gbank()
                P.op("pe", [_mm(pb2[0:64, :], wq[:, kc, h * 256 + 192:h * 256 + 256], qnb[s][:, kc, :], kc == 0, kc == 5)
                            for kc in range(6)], reads=[wq, qnb[s]], writes=[pb2])
                P.op("dve", _tt(rb[:, :], pb2[0:64, :], sqb[s][:, :], ALU.mult), reads=[pb2, sqb[s]], writes=[rb])
                P.op("pool", _tt(qrT[u][:, :], ra[:, :], rb[:, :], ALU.add), reads=[ra, rb], writes=[qrT[u]])
                nkt = 4 * i + 4
                ob, mb = op_[u], sm[u]

                def s_tile(kt):
                    j = kt - 4 * i
                    c0 = max(j, 0) * 128
                    sb_ = sp[kt % 2]
                    ksl = slice(kt * 128, (kt + 1) * 128)
                    P.op("pe", [_mm(sb_[:, c0:512], knT[:, hl, ksl], qT[u][:, c0:512], True, False),
                                _mm(sb_[:, c0:512], krT[:, ksl], qrT[u][:, c0:512], False, True)],
                         reads=[knT, krT, qT[u], qrT[u]], writes=[sb_])
                    pt = pT[pn[0] % 3]
                    pn[0] += 1
                    P.op("act", _act(pt[:, c0:512], sb_[:, c0:512], AF.Exp, scale=SCALE), reads=[sb_], writes=[pt])
                    if j >= 0:
                        P.op("pool", _tt(pt[:, c0:c0 + 128], pt[:, c0:c0 + 128], tri, ALU.mult),
                             reads=[pt, mskb], writes=[pt])
                    return pt, c0

                def pv_tile(kt, pt, c0):
                    P.op("pe", [_mm(ob[:, c0:512], V[:, kt, hl * 128:(hl + 1) * 128], pt[:, c0:512], kt == 0, kt == nkt - 1),
                                _mm(mb[:, c0:512], ones, pt[:, c0:512], kt == 0, kt == nkt - 1)],
                         reads=[V, pt, mskb], writes=[ob, mb])

                prev = s_tile(0)
                for kt in range(nkt):
                    nxt = s_tile(kt + 1) if kt + 1 < nkt else None
                    pv_tile(kt, *prev)
                    prev = nxt
                P.op("dve", lambda e, mb=mb: e.reciprocal(out=rsum[:, :], in_=mb[:, :]), reads=[mb], writes=[rsum])
                P.op("dve", _tt(ost[u][:, :], ob[:, :], rsum[:, :], ALU.mult), reads=[ob, rsum], writes=[ost[u]])
                P.dma(_dma(oTv[:, h, bsl], ost[u][:, :]), reads=[ost[u]], writes=[oT_d], key="ost%d" % u)
        P.wait_all("sp", [oT_d.w])
        P.emit()
    return nc


def run_C(resB, b_w_uq, kv_w_up, nblk=16, npair=2):
    cos, sin = _rope_tables(32)
    cq = np.ascontiguousarray(np.concatenate([cos, cos], axis=1).T)
    sq = np.ascontiguousarray(np.concatenate([-sin, sin], axis=1).T)
    idx = np.arange(128)
    msk = np.concatenate([(idx[:, None] <= idx[None, :]).astype(np.float32), np.ones((128, 128), np.float32)], axis=1)
    wuq = b_w_uq[0].reshape(768, 16, 192)
    wup = kv_w_up.reshape(512, 16, 256)
    in_maps = []
    for core in range(NCORES):
        b, g = divmod(core, 4)
        hs = slice(4 * g, 4 * g + 4)
        w = wuq[:, hs]
        wq = np.concatenate([w[:, :, 0:128], w[:, :, 128:192], w[:, :, 160:192], w[:, :, 128:160]], axis=2).reshape(768, 1024)
        wu = np.concatenate([wup[:, hs, 0:128].reshape(512, 512), wup[:, hs, 128:256].reshape(512, 512)], axis=1)
        cat = lambda k: np.ascontiguousarray(np.concatenate([np.asarray(resB[b * 4 + j][k]) for j in range(4)], axis=1))
        in_maps.append({"qnT": cat("qnT"), "latT": cat("latT"), "krT": cat("krT"), "wq": np.ascontiguousarray(wq),
                        "wu": np.ascontiguousarray(wu), "cq": cq, "sq": sq, "msk": msk})
    nc = build_C(nblk, npair)
    res = run_bass_kernel_spmd(nc, in_maps, core_ids=list(range(NCORES)))
    return [r["oT"] for r in res.results]


def build_D(ngrp=4):
    nc = bass.Bass("TRN2", target_bir_lowering=False)
    with contextlib.ExitStack() as stack:
        P = Prog(nc, stack)
        oT_d = P.dram("oTin", [2048, TB], F32, "ExternalInput")
        sg_d = P.dram("sgT", [2048, TB], BF16, "ExternalInput")
        x1_d = P.dram("x1", [TB, D], F32, "ExternalInput")
        wo_d = P.dram("wo", [2048, D], F32, "ExternalInput")
        vec_d = P.dram("vec", [128, 2048], F32, "ExternalInput")
        out_d = P.dram("out", [TB, D], F32, "ExternalOutput")

        wo = P.sb("wo", [128, 16, 1024], BF16)
        wst = [P.sb("wst%d" % i, [128, 1024], F32) for i in range(2)]
        vec = P.sb("vec", [128, 2048], F32)
        of = [P.sb("of%d" % i, [128, 16, 512], F32) for i in range(2)]
        sgb = [P.sb("sgb%d" % i, [128, 16, 512], BF16) for i in range(2)]
        g1 = [P.sb("g1%d" % i, [128, 16, 512], BF16) for i in range(2)]
        xs = [P.sb("xs%d" % i, [128, 1024], F32) for i in range(2)]
        os_ = [P.sb("os%d" % i, [128, 1024], F32) for i in range(2)]
        T = {"z": P.sb("z", [128, 1024], F32), "stt": P.sb("stt", [128, 12], F32), "mv": P.sb("mv", [128, 2], F32),
             "rstd": P.sb("rstd", [128, 1], F32), "vec": vec}
        T["yp"] = [P.ps("y0"), P.ps("y1")]
        P.dma(_dma(vec[:, :], vec_d[:, :]), writes=[vec], key="vec")
        _load_w(P, wo, wo_d, 16, 1024, wst)
        gvec, bvec = vec[:, 0:1024], vec[:, 1024:2048]
        oTv = oT_d.t.rearrange("(k p) t -> p k t", p=128)
        sgv = sg_d.t.rearrange("(k p) t -> p k t", p=128)
        for gi in range(ngrp):
            s = gi % 2
            gsl = slice(gi * 512, (gi + 1) * 512)
            P.dma(_dma(of[s][:, :, :], oTv[:, :, gsl]), writes=[of[s]], key="of%d" % s)
            P.dma(_dma(sgb[s][:, :, :], sgv[:, :, gsl]), writes=[sgb[s]], key="sgb%d" % s)
            for q in range(4):
                P.op("dve" if q % 2 == 0 else "pool",
                     _tt(g1[s][:, q * 4:(q + 1) * 4, :], of[s][:, q * 4:(q + 1) * 4, :], sgb[s][:, q * 4:(q + 1) * 4, :], ALU.mult),
                     reads=[of[s], sgb[s]], writes=[g1[s]])
            for t in range(4):
                ti = gi * 4 + t
                u = ti % 2
                tsl = slice(t * 128, (t + 1) * 128)
                P.dma(_dma(xs[u][:, :], x1_d[ti * 128:(ti + 1) * 128, :]), writes=[xs[u]], key="xs%d" % u)
                _outproj_ln(P, T, g1[s], tsl, wo, xs[u], gvec, bvec, os_[u])
                P.dma(_dma(out_d[ti * 128:(ti + 1) * 128, :], os_[u][:, :]), reads=[os_[u]], writes=[out_d], key="os%d" % u)
        P.wait_all("sp", [out_d.w])
        P.emit()
    return nc


def run_D(oTs, resB, b_w_out, ln_g, ln_b, ngrp=4):
    vec = np.concatenate([_bcast(ln_g[1]), _bcast(ln_b[1])], axis=1)
    in_maps = []
    for core in range(NCORES):
        b, j = divmod(core, 4)
        tsl = slice(j * TB, (j + 1) * TB)
        oTin = np.concatenate([np.asarray(oTs[b * 4 + g])[:, tsl] for g in range(4)], axis=0)
        in_maps.append({"oTin": np.ascontiguousarray(oTin), "sgT": np.asarray(resB[core]["sgT"]),
                        "x1": np.asarray(resB[core]["x1"]), "wo": np.ascontiguousarray(b_w_out[0]), "vec": vec})
    nc = build_D(ngrp)
    res = run_bass_kernel_spmd(nc, in_maps, core_ids=list(range(NCORES)))
    return [r["out"] for r in res.results]


def kernel(x, a_w_in, a_w_out, b_w_in, b_q_norm, b_w_uq, b_w_out, kv_w_down, kv_norm, kv_w_up, ln_g, ln_b):
    x = np.asarray(x, np.float32)
    args = [np.asarray(a, np.float32) for a in (a_w_in, a_w_out, b_w_in, b_q_norm, b_w_uq, b_w_out, kv_w_down, kv_norm,
                                                 kv_w_up, ln_g, ln_b)]
    a_w_in, a_w_out, b_w_in, b_q_norm, b_w_uq, b_w_out, kv_w_down, kv_norm, kv_w_up, ln_g, ln_b = args
    GTs = run_A(x, a_w_in)
    resB = run_B(GTs, x, a_w_out, kv_w_down, b_w_in, ln_g, ln_b, kv_norm, b_q_norm)
    oTs = run_C(resB, b_w_uq, kv_w_up)
    outs = run_D(oTs, resB, b_w_out, ln_g, ln_b)
    out = np.zeros((NB, S, D), np.float32)
    for core in range(NCORES):
        b, j = divmod(core, 4)
        out[b, j * TB:(j + 1) * TB] = np.asarray(outs[core])
    return out
```
